# Optimizing a Trainium2 kernel written in Bass

```python
import math
import jax
import jax.numpy as jnp
from jax import lax
import numpy as np

D_MODEL = 1024
BATCH = 16
SEQ = 2048
DEPTH = 4
DEC_BATCH = 8
DEC_SEQ = 64
PAST_LEN = 4096

CHUNK = 64
GDN_DK = 128
GDN_DV = 128
GDN_HEADS = D_MODEL // (2 * GDN_DV)
GDN_WIDTH = GDN_HEADS * GDN_DV
GDN_QKV = 2 * GDN_HEADS * GDN_DK + GDN_WIDTH
CONV_W = 4
GDN_PROJ = GDN_QKV + GDN_WIDTH + 2 * GDN_HEADS
RW_HD = 64
RW_HEADS = D_MODEL // (2 * RW_HD)
RW_WIDTH = RW_HEADS * RW_HD
W_LORA = 64
A_LORA = 64
G_LORA = 128
RW_PROJ = 3 * RW_WIDTH + W_LORA + A_LORA + G_LORA
P_TOT = GDN_PROJ + RW_PROJ
MIX_WIDTH = GDN_WIDTH + RW_WIDTH
D_FF = 4 * D_MODEL
DN_ALPHA = (2 * DEPTH) ** 0.25
DN_BETA = (8 * DEPTH) ** -0.25
LN_EPS = 1e-5
GN_EPS = 64e-5
NORM_EPS = 1e-6

kernel_name = 'hymba_gdn_rwkv7_deepnorm_adaln_stream'


def _l2norm(t):
    t = t.astype(jnp.float32)
    return t * lax.rsqrt(jnp.sum(t * t, axis=-1, keepdims=True) + NORM_EPS)


def _layer_norm(t, w, b, eps):
    tf = t.astype(jnp.float32)
    mu = jnp.mean(tf, axis=-1, keepdims=True)
    var = jnp.mean(jnp.square(tf - mu), axis=-1, keepdims=True)
    return ((tf - mu) * lax.rsqrt(var + eps) * w + b).astype(t.dtype)


def _rms_norm(t, w):
    tf = t.astype(jnp.float32)
    return tf * lax.rsqrt(jnp.mean(tf * tf, axis=-1, keepdims=True) + NORM_EPS) * w


def _causal_conv(t, buf, w):
    n = t.shape[1]
    tp = jnp.concatenate([buf.astype(t.dtype), t], axis=1)
    y = tp[:, 0:n] * w[0]
    for j in range(1, CONV_W):
        y = y + tp[:, j:j + n] * w[j]
    return y, tp[:, n:]


def _token_shift(t, prev, mu):
    tp = jnp.concatenate([prev[:, None, :].astype(t.dtype), t[:, :-1]], axis=1)
    return t + (tp - t) * mu, t[:, -1]


def _gated_delta_chunked(q, k, v, log_a, beta, s0, chunk):
    bsz, seq, nh, dk = q.shape
    dv = v.shape[-1]
    nb = seq // chunk

    def blk(t):
        t = t.astype(jnp.float32).reshape((bsz, nb, chunk) + t.shape[2:])
        return jnp.moveaxis(t, 3, 2)

    q, k, v, log_a, beta = blk(q), blk(k), blk(v), blk(log_a), blk(beta)
    g = jnp.cumsum(log_a, axis=-1)
    idx = jnp.arange(chunk)
    causal = idx[:, None] >= idx[None, :]
    strict = idx[:, None] > idx[None, :]
    gam = jnp.exp(jnp.where(causal, g[..., :, None] - g[..., None, :], -jnp.inf))
    a_mat = jnp.where(strict, beta[..., :, None] * jnp.einsum('bnhtd,bnhid->bnhti', k, k) * gam, 0.0)
    eye = jnp.eye(chunk, dtype=jnp.float32)
    rhs = jnp.concatenate([v * beta[..., None], k * (beta * jnp.exp(g))[..., None]], axis=-1)
    sol = lax.linalg.triangular_solve(a_mat + eye, rhs, left_side=True, lower=True, unit_diagonal=True)
    u0, wk = sol[..., :dv], sol[..., dv:]
    qk = jnp.where(causal, jnp.einsum('bnhtd,bnhid->bnhti', q, k) * gam, 0.0)
    g_last = g[..., -1:]
    q_dec = q * jnp.exp(g)[..., None]
    k_dec = k * jnp.exp(g_last - g)[..., None]
    d_last = jnp.exp(g_last[..., 0])

    def step(s, xs):
        u0_c, wk_c, qk_c, qd_c, kd_c, dl_c = xs
        u = u0_c - jnp.einsum('bhtk,bhkv->bhtv', wk_c, s)
        o = jnp.einsum('bhtk,bhkv->bhtv', qd_c, s) + jnp.einsum('bhti,bhiv->bhtv', qk_c, u)
        s = s * dl_c[..., None, None] + jnp.einsum('bhtk,bhtv->bhkv', kd_c, u)
        return s, o

    xs = tuple(jnp.moveaxis(t, 1, 0) for t in (u0, wk, qk, q_dec, k_dec, d_last))
    s_fin, o = lax.scan(step, s0.astype(jnp.float32), xs)
    o = jnp.transpose(o, (1, 0, 3, 2, 4)).reshape(bsz, seq, nh, dv)
    return o, s_fin


def _rwkv7_scan(r, decay, k, v, kk, a, s0):
    def tm(t):
        return jnp.moveaxis(t.astype(jnp.float32), 1, 0)

    def step(s, xs):
        r_t, w_t, k_t, v_t, kk_t, a_t = xs
        sa = jnp.einsum('bhij,bhj->bhi', s, -kk_t)
        s = (s * w_t[:, :, None, :] + sa[..., :, None] * (kk_t * a_t)[:, :, None, :]
             + v_t[..., :, None] * k_t[:, :, None, :])
        return s, jnp.einsum('bhij,bhj->bhi', s, r_t)

    s_fin, o = lax.scan(step, s0.astype(jnp.float32), tuple(tm(t) for t in (r, decay, k, v, kk, a)))
    return jnp.moveaxis(o, 0, 1), s_fin


def _mixer(h, l, p, s_gdn, s_conv, s_rw, s_shift, chunk):
    bsz, seq, _ = h.shape
    proj = jnp.einsum('btd,dp->btp', h, p['w_in'][l])
    o1 = GDN_QKV
    o2 = o1 + GDN_WIDTH
    o3 = o2 + GDN_HEADS
    qkv, z, ga, gb, rw = proj[..., :o1], proj[..., o1:o2], proj[..., o2:o3], proj[..., o3:GDN_PROJ], proj[..., GDN_PROJ:]
    qkv, new_conv = _causal_conv(qkv, s_conv, p['gdn_conv_w'][l])
    qkv = jax.nn.silu(qkv)
    nk = GDN_HEADS * GDN_DK
    q = _l2norm(qkv[..., :nk].reshape(bsz, seq, GDN_HEADS, GDN_DK)) * (GDN_DK ** -0.5)
    k = _l2norm(qkv[..., nk:2 * nk].reshape(bsz, seq, GDN_HEADS, GDN_DK))
    v = qkv[..., 2 * nk:].reshape(bsz, seq, GDN_HEADS, GDN_DV)
    log_a = -jnp.exp(p['gdn_a_log'][l].astype(jnp.float32)) * jax.nn.softplus(ga.astype(jnp.float32) + p['gdn_dt_bias'][l])
    beta = jax.nn.sigmoid(gb.astype(jnp.float32))
    o_g, new_gdn = _gated_delta_chunked(q, k, v, log_a, beta, s_gdn, chunk)
    o_g = _rms_norm(o_g, p['gdn_norm_w'][l]) * jax.nn.silu(z.astype(jnp.float32)).reshape(bsz, seq, GDN_HEADS, GDN_DV)
    o_g = o_g.reshape(bsz, seq, GDN_WIDTH)
    rw, new_shift = _token_shift(rw, s_shift, p['rwkv_mu'][l])
    c1 = RW_WIDTH
    c2 = 2 * RW_WIDTH
    c3 = 3 * RW_WIDTH
    c4 = c3 + W_LORA
    c5 = c4 + A_LORA
    r, kr, vr = rw[..., :c1], rw[..., c1:c2], rw[..., c2:c3]
    xw, xa, xg = rw[..., c3:c4], rw[..., c4:c5], rw[..., c5:]
    w_log = -jax.nn.softplus(-(p['rwkv_w0'][l] + jnp.tanh(xw) @ p['rwkv_w2'][l]).astype(jnp.float32)) - 0.5
    decay = jnp.exp(-jnp.exp(w_log))
    a = jax.nn.sigmoid((p['rwkv_a0'][l] + xa @ p['rwkv_a2'][l]).astype(jnp.float32))
    g = jax.nn.sigmoid(xg) @ p['rwkv_g2'][l]

    def heads(t):
        return t.reshape(bsz, seq, RW_HEADS, RW_HD)

    kk = _l2norm(heads(kr * p['rwkv_kk'][l]))
    kr = kr * (1.0 + (a - 1.0) * p['rwkv_ka'][l])
    o_r, new_rw = _rwkv7_scan(heads(r), heads(decay), heads(kr), heads(vr), kk, heads(a), s_rw)
    o_r = _layer_norm(o_r, p['rwkv_ln_w'][l].reshape(RW_HEADS, RW_HD), p['rwkv_ln_b'][l].reshape(RW_HEADS, RW_HD), GN_EPS)
    bonus = jnp.sum(heads(r * kr * p['rwkv_rk'][l]).astype(jnp.float32), axis=-1, keepdims=True) * heads(vr)
    o_r = (o_r + bonus).reshape(bsz, seq, RW_WIDTH) * g
    y = jnp.concatenate([o_g, o_r], axis=-1).astype(h.dtype) @ p['w_out'][l]
    return y, new_gdn, new_conv, new_rw, new_shift


def _trunk(x, c, st_gdn, st_conv, st_rw, st_shift, p, chunk):
    outs = ([], [], [], [])
    for l in range(DEPTH):
        mod = jnp.einsum('bd,de->be', jax.nn.silu(c), p['w_ada'][l]) + p['b_ada'][l]
        sh1, sc1, g1, sh2, sc2, g2 = jnp.split(mod[:, None, :], 6, axis=-1)
        h = x * (1.0 + sc1) + sh1
        y, n_gdn, n_conv, n_rw, n_shift = _mixer(h, l, p, st_gdn[l], st_conv[l], st_rw[l], st_shift[l], chunk)
        for lst, s in zip(outs, (n_gdn, n_conv, n_rw, n_shift)):
            lst.append(s.astype(x.dtype))
        x = _layer_norm(DN_ALPHA * x + g1 * y, p['ln1_w'][l], p['ln1_b'][l], LN_EPS)
        h = x * (1.0 + sc2) + sh2
        f = jnp.square(jax.nn.relu(h @ p['w_ff1'][l])) @ p['w_ff2'][l]
        x = _layer_norm(DN_ALPHA * x + g2 * f, p['ln2_w'][l], p['ln2_b'][l], LN_EPS)
    return x, jnp.stack(outs[0]), jnp.stack(outs[1]), jnp.stack(outs[2]), jnp.stack(outs[3])


def setup_inputs(seed: int = 0) -> dict:
    key = jax.random.key(seed)
    ks = iter(jax.random.split(key, 48))

    def nrm(shape, s):
        return jax.random.normal(next(ks), shape, jnp.float32) * s

    def uni(shape, lo, hi):
        return jax.random.uniform(next(ks), shape, jnp.float32, lo, hi)

    L = DEPTH
    D = D_MODEL
    x_prompt = nrm((BATCH, SEQ, D), 1.0)
    x_sample = nrm((DEC_BATCH, DEC_SEQ, D), 1.0)
    c_prompt = nrm((BATCH, D), 1.0)
    c_sample = nrm((DEC_BATCH, D), 1.0)
    state_gdn = nrm((L, DEC_BATCH, GDN_HEADS, GDN_DK, GDN_DV), 0.1)
    state_gdn_conv = nrm((L, DEC_BATCH, CONV_W - 1, GDN_QKV), 1.0)
    state_rwkv = nrm((L, DEC_BATCH, RW_HEADS, RW_HD, RW_HD), 1.0)
    state_rwkv_shift = nrm((L, DEC_BATCH, RW_PROJ), 1.0)
    w_ada = nrm((L, D, 6 * D), 0.5 * D ** -0.5)
    b_ada = nrm((L, 6 * D), 0.02)
    w_in = nrm((L, D, P_TOT), D ** -0.5)
    gdn_conv_w = nrm((L, CONV_W, GDN_QKV), CONV_W ** -0.5)
    gdn_a_log = jnp.log(uni((L, GDN_HEADS), 1.0, 16.0))
    dt = jnp.exp(uni((L, GDN_HEADS), math.log(1e-3), math.log(1e-1)))
    gdn_dt_bias = dt + jnp.log(-jnp.expm1(-dt))
    gdn_norm_w = 1.0 + nrm((L, GDN_DV), 0.02)
    rwkv_mu = uni((L, RW_PROJ), 0.0, 1.0)
    rwkv_w0 = uni((L, RW_WIDTH), -6.5, -1.5)
    rwkv_w2 = nrm((L, W_LORA, RW_WIDTH), 0.5 * W_LORA ** -0.5)
    rwkv_a0 = nrm((L, RW_WIDTH), 0.1)
    rwkv_a2 = nrm((L, A_LORA, RW_WIDTH), A_LORA ** -0.5)
    rwkv_g2 = nrm((L, G_LORA, RW_WIDTH), G_LORA ** -0.5)
    rwkv_kk = 0.85 + nrm((L, RW_WIDTH), 0.02)
    rwkv_ka = 1.0 + nrm((L, RW_WIDTH), 0.02)
    rwkv_rk = nrm((L, RW_WIDTH), 0.1)
    rwkv_ln_w = 1.0 + nrm((L, RW_WIDTH), 0.02)
    rwkv_ln_b = nrm((L, RW_WIDTH), 0.02)
    w_out = nrm((L, MIX_WIDTH, D), MIX_WIDTH ** -0.5 * DN_BETA)
    ln1_w = 1.0 + nrm((L, D), 0.02)
    ln1_b = nrm((L, D), 0.02)
    w_ff1 = nrm((L, D, D_FF), D ** -0.5)
    w_ff2 = nrm((L, D_FF, D), D_FF ** -0.5 * DN_BETA)
    ln2_w = 1.0 + nrm((L, D), 0.02)
    ln2_b = nrm((L, D), 0.02)
    return {'x_prompt': x_prompt, 'x_sample': x_sample, 'c_prompt': c_prompt, 'c_sample': c_sample,
            'state_gdn': state_gdn, 'state_gdn_conv': state_gdn_conv, 'state_rwkv': state_rwkv,
            'state_rwkv_shift': state_rwkv_shift, 'w_ada': w_ada, 'b_ada': b_ada, 'w_in': w_in,
            'gdn_conv_w': gdn_conv_w, 'gdn_a_log': gdn_a_log, 'gdn_dt_bias': gdn_dt_bias,
            'gdn_norm_w': gdn_norm_w, 'rwkv_mu': rwkv_mu, 'rwkv_w0': rwkv_w0, 'rwkv_w2': rwkv_w2,
            'rwkv_a0': rwkv_a0, 'rwkv_a2': rwkv_a2, 'rwkv_g2': rwkv_g2, 'rwkv_kk': rwkv_kk,
            'rwkv_ka': rwkv_ka, 'rwkv_rk': rwkv_rk, 'rwkv_ln_w': rwkv_ln_w, 'rwkv_ln_b': rwkv_ln_b,
            'w_out': w_out, 'ln1_w': ln1_w, 'ln1_b': ln1_b, 'w_ff1': w_ff1, 'w_ff2': w_ff2,
            'ln2_w': ln2_w, 'ln2_b': ln2_b}


def reference(x_prompt, x_sample, c_prompt, c_sample, state_gdn, state_gdn_conv, state_rwkv,
              state_rwkv_shift, w_ada, b_ada, w_in, gdn_conv_w, gdn_a_log, gdn_dt_bias, gdn_norm_w,
              rwkv_mu, rwkv_w0, rwkv_w2, rwkv_a0, rwkv_a2, rwkv_g2, rwkv_kk, rwkv_ka, rwkv_rk,
              rwkv_ln_w, rwkv_ln_b, w_out, ln1_w, ln1_b, w_ff1, w_ff2, ln2_w, ln2_b):
    p = dict(w_ada=w_ada, b_ada=b_ada, w_in=w_in, gdn_conv_w=gdn_conv_w, gdn_a_log=gdn_a_log,
             gdn_dt_bias=gdn_dt_bias, gdn_norm_w=gdn_norm_w, rwkv_mu=rwkv_mu, rwkv_w0=rwkv_w0,
             rwkv_w2=rwkv_w2, rwkv_a0=rwkv_a0, rwkv_a2=rwkv_a2, rwkv_g2=rwkv_g2, rwkv_kk=rwkv_kk,
             rwkv_ka=rwkv_ka, rwkv_rk=rwkv_rk, rwkv_ln_w=rwkv_ln_w, rwkv_ln_b=rwkv_ln_b, w_out=w_out,
             ln1_w=ln1_w, ln1_b=ln1_b, w_ff1=w_ff1, w_ff2=w_ff2, ln2_w=ln2_w, ln2_b=ln2_b)
    bp = x_prompt.shape[0]
    dt = x_prompt.dtype
    y_prompt, p_gdn, p_conv, p_rw, p_shift = _trunk(
        x_prompt, c_prompt,
        jnp.zeros((DEPTH, bp, GDN_HEADS, GDN_DK, GDN_DV), dt),
        jnp.zeros((DEPTH, bp, CONV_W - 1, GDN_QKV), dt),
        jnp.zeros((DEPTH, bp, RW_HEADS, RW_HD, RW_HD), dt),
        jnp.zeros((DEPTH, bp, RW_PROJ), dt),
        p, CHUNK)
    y_sample, s_gdn, s_conv, s_rw, s_shift = _trunk(
        x_sample, c_sample, state_gdn, state_gdn_conv, state_rwkv, state_rwkv_shift,
        p, x_sample.shape[1])
    return (y_prompt, y_sample, p_gdn, p_conv, p_rw, p_shift, s_gdn, s_conv, s_rw, s_shift)
```

```python
import math
from contextlib import ExitStack
import numpy as np
import concourse.bass as bass
import concourse.mybir as mybir
from concourse.bass_utils import run_bass_kernel_spmd

F32 = mybir.dt.float32
BF16 = mybir.dt.bfloat16
AF = mybir.ActivationFunctionType
ALU = mybir.AluOpType
AX = mybir.AxisListType

L = 4
D = 1024
SEQ = 2048
DSEQ = 64
TM = 256
NCHM = TM // 64
ALPHA = (2 * L) ** 0.25
LN_EPS = 1e-5
GN_EPS = 64e-5
NORM_EPS = 1e-6
NG = 26
NV = 119
ENGS = ("pe", "act", "dve", "pool", "sp")


class Buf:
    __slots__ = ("w", "r")

    def __init__(self):
        self.w = None
        self.r = {}


class Sem:
    __slots__ = ("h", "v", "owner")

    def __init__(self, h, owner):
        self.h = h
        self.v = 0
        self.owner = owner


class TT:
    def __init__(self, t, bufs=None):
        self.t = t
        self.bufs = bufs if bufs is not None else [Buf()]

    def __getitem__(self, k):
        return self.t[k]


def _bufs(xs):
    out = []
    for x in xs:
        if isinstance(x, Buf):
            out.append(x)
        else:
            out.extend(x.bufs)
    return out


class Prog:
    def __init__(self, nc, es):
        self.nc = nc
        self.es = es
        self.q = {e: [] for e in ENGS}
        self.nsem = 0
        self.esem = {e: self.new_sem(e) for e in ENGS if e != "sp"}
        self.waited = {e: {} for e in ENGS}
        self.nops = 0
        self.cap = None

    def new_sem(self, owner):
        self.nsem += 1
        return Sem(self.es.enter_context(self.nc.semaphore("s%d" % self.nsem)), owner)

    def op(self, eng, fn, R=(), W=(), dsem=None):
        if self.cap is not None:
            self.cap.append((0, eng, fn, R, W, dsem))
            return
        self._op(eng, fn, R, W, dsem)

    def defer(self, f):
        if self.cap is not None:
            self.cap.append((1, f))
        else:
            f()

    def _emit(self, it):
        if it[0] == 0:
            self._op(*it[1:])
        else:
            it[1]()

    def merge_replay(self, A, B):
        na, nb = len(A), len(B)
        j = 0
        for i, it in enumerate(A):
            self._emit(it)
            while j < nb and (j + 1) * na <= (i + 1) * nb:
                self._emit(B[j])
                j += 1
        while j < nb:
            self._emit(B[j])
            j += 1

    def _op(self, eng, fn, R=(), W=(), dsem=None):
        R = _bufs(R)
        W = _bufs(W)
        deps = {}

        def add(s, v):
            if deps.get(s, 0) < v:
                deps[s] = v

        for b in R:
            if b.w is not None:
                add(*b.w)
        for b in W:
            if b.w is not None:
                add(*b.w)
            for s, v in b.r.items():
                add(s, v)
        wd = self.waited[eng]
        for s, v in deps.items():
            if eng == "pe" and s.owner == "pe":
                continue
            if wd.get(s, 0) < v:
                self.q[eng].append((0, s, v))
                wd[s] = v
        if dsem is None:
            s = self.esem[eng]
            inc = 1
        else:
            s = dsem
            inc = 16
        s.v += inc
        self.q[eng].append((1, fn, s, inc))
        self.nops += 1
        for b in R:
            if b.r.get(s, 0) < s.v:
                b.r[s] = s.v
        for b in W:
            b.w = (s, s.v)
            b.r = {}
        if dsem is None and s.v >= 50000:
            self.esem[eng] = self.new_sem(eng)

    def wait_on(self, eng, s, v):
        if self.waited[eng].get(s, 0) < v:
            self.q[eng].append((0, s, v))
            self.waited[eng][s] = v

    def mm(self, out, lhsT, rhs, start=True, stop=True, R=(), W=()):
        self.op("pe", lambda e: e.matmul(out, lhsT=lhsT, rhs=rhs, start=start, stop=stop), R, W)

    def tr(self, out, in_, ident, R=(), W=()):
        self.op("pe", lambda e: e.transpose(out, in_, ident), R, W)

    def act(self, out, in_, func, bias=None, scale=None, R=(), W=()):
        kw = {}
        if bias is not None:
            kw["bias"] = bias
        if scale is not None:
            kw["scale"] = scale
        self.op("act", lambda e: e.activation(out=out, in_=in_, func=func, **kw), R, W)

    def tt(self, eng, out, in0, in1, op, R=(), W=()):
        self.op(eng, lambda e: e.tensor_tensor(out=out, in0=in0, in1=in1, op=op), R, W)

    def ts(self, eng, out, in0, s1, s2, op0, op1=None, R=(), W=()):
        if op1 is None:
            self.op(eng, lambda e: e.tensor_scalar(out=out, in0=in0, scalar1=s1, scalar2=None, op0=op0), R, W)
        else:
            self.op(eng, lambda e: e.tensor_scalar(out=out, in0=in0, scalar1=s1, scalar2=s2, op0=op0, op1=op1), R, W)

    def stt(self, eng, out, in0, scalar, in1, op0, op1, R=(), W=()):
        self.op("dve", lambda e: e.scalar_tensor_tensor(out=out, in0=in0, scalar=scalar, in1=in1, op0=op0, op1=op1), R, W)

    def copy(self, eng, out, in_, R=(), W=()):
        if eng == "act":
            self.op("act", lambda e: e.activation(out=out, in_=in_, func=AF.Identity), R, W)
        else:
            self.op(eng, lambda e: e.tensor_copy(out=out, in_=in_), R, W)

    def memset(self, eng, out, val, W=()):
        self.op(eng, lambda e: e.memset(out, val), (), W)

    def rsum(self, eng, out, in_, R=(), W=()):
        self.op(eng, lambda e: e.reduce_sum(out=out, in_=in_, axis=AX.X), R, W)

    def dma(self, q, out, in_, sem, R=(), W=()):
        self.op(q, lambda e: e.dma_start(out=out, in_=in_), R, W, dsem=sem)


def bc(ap, shape):
    return ap.to_broadcast(list(shape))


def build_nc():
    nc = bass.Bass("TRN2", target_bir_lowering=False)

    def din(name, shape, dt=F32):
        return nc.dram_tensor(name, list(shape), dt, kind="ExternalInput").ap()

    def dout(name, shape, dt=F32):
        return nc.dram_tensor(name, list(shape), dt, kind="ExternalOutput").ap()

    xp = din("xp", [2, D, SEQ])
    xs = din("xs", [D, DSEQ])
    cT = din("cT", [128, 24])
    sgdn = din("sgdn", [L, 128, 4 * 128])
    sconv = din("sconv", [L, 128, 36])
    srw = din("srw", [L, 128, 4 * 128])
    sshift = din("sshift", [L, 128, 14])
    wpack = din("wpack", [L, NG, 128, 4096])
    wgab = din("wgab", [128, L * 64])
    wada = din("wada", [L, 12, 128, 4096])
    bada = din("bada", [128, L * 48])
    vecs = din("vecs", [128, L * NV])
    rowc = din("rowc", [64, L * 8])
    w0b = din("w0b", [L, 64, 512])
    w2a2 = din("w2a2", [L, 128, 512])
    g2w = din("g2w", [L, 128, 512])
    c64 = din("c64", [64, 17 * 64])
    c128 = din("c128", [128, 4 * 128])

    ypT = dout("ypT", [2, D, SEQ])
    ysT = dout("ysT", [D, DSEQ])
    o_gdn = dout("o_gdn", [3, L, 128, 512])
    o_conv = dout("o_conv", [3, L, 128, 36])
    o_rw = dout("o_rw", [3, L, 128, 512])
    o_shift = dout("o_shift", [3, L, 128, 14])
    wbf_h = nc.dram_tensor("wbf", [L, NG, 128, 4096], BF16, kind="Internal")
    wbf = wbf_h.ap()

    with ExitStack() as es:
        P = Prog(nc, es)

        def sb(name, shape, dt=F32):
            return TT(es.enter_context(nc.sbuf_tensor(name, list(shape), dt)))

        xT = sb("xT", [128, 8, TM])
        hT = sb("hT", [128, 8, TM], BF16)
        mixT = sb("mixT", [128, 8, TM], BF16)
        zs = sb("zs", [128, 4, TM], BF16)
        wb = [sb("wb%d" % i, [128, 4096], BF16) for i in range(2)]
        C64 = sb("C64", [64, 17, 64])
        C128 = sb("C128", [128, 4, 128])
        onesb = sb("onesb", [128, 128], BF16)
        blkb = sb("blkb", [128, 128], BF16)
        Sg = sb("Sg", [128, 16, 128])
        Mb = sb("Mb", [128, 16, 128])
        cst = sb("cst", [128, L, 36])
        sst = sb("sst", [128, L, 14])
        VEC = sb("VEC", [128, L, NV])
        OKA = sb("OKA", [128, L, 4])
        ROWC = sb("ROWC", [64, L, 8])
        NEA = sb("NEA", [64, L, 4])
        W0B = sb("W0B", [64, 512])
        W2A2 = sb("W2A2", [128, 512])
        G2W = sb("G2W", [128, 512])
        WGAB = sb("WGAB", [128, L, 8, 8])
        WGABb = sb("WGABb", [128, L, 8, 8], BF16)
        MOD = sb("MOD", [128, L, 48, 3])
        BADA = sb("BADA", [128, L, 48])
        CT = sb("CT", [128, 8, 3])
        SCT = sb("SCT", [128, 8, 3])
        sm = {n: sb("sm_" + n, [64, 16]) for n in
              ("x1", "ax", "e1", "l1", "mx", "sp", "la", "beta", "lnb", "gcol", "eg", "cb", "ed", "gpl", "dd")}
        dl128 = sb("dl128", [128, 16])
        st8 = {n: sb("st_" + n, [64, 8]) for n in ("sum", "ssq", "mean", "msq", "var", "r1", "rstd")}
        WC = sb("WC", [128, 4, NCHM])
        tmpA = [sb("tmpA%d" % i, [128, TM]) for i in range(2)]
        tmpB = [sb("tmpB%d" % i, [128, TM]) for i in range(2)]
        lnt = {"mean": tmpA[0], "msq": tmpA[1], "var": tmpB[0], "rstd": tmpB[1]}
        sqb = sb("sqb", [128, 4, TM], BF16)
        txw = sb("txw", [64, TM])
        sxg = sb("sxg", [128, TM])
        tmpd = sb("tmpd", [64, 512])
        QKB = sb("QKB", [128, 12, TM], BF16)
        identb = sb("identb", [128, 128], BF16)
        bw = [sb("bw%d" % i, [128, 512], BF16) for i in range(9)]
        Sb = sb("Sb", [128, 4, 128], BF16)
        Mbb = sb("Mbb", [128, 4, 128], BF16)
        RB = sb("RB", [128, 7, 4, TM], BF16)
        ONB = sb("ONB", [64, 2048], BF16)

        def b3(i, a, n):
            return bw[i].t[0:64, 0:a * n].rearrange("p (a b) -> p a b", a=a)

        QW = 12 * (TM + 3)
        RW = 14 * (TM + 1)
        SL = 4 * TM
        NWS = 15
        acols = QW + RW + 8 * SL + (NWS - 10) * 512 + 64
        arena = es.enter_context(nc.sbuf_tensor("arena", [128, acols], F32))
        off = [0]

        def region(n):
            r = TT(arena[:, off[0]:off[0] + n])
            off[0] += n
            return r

        qraw = region(QW)
        rraw = region(RW)
        slab = [region(SL) for _ in range(8)]
        ws = [sb("iw%d" % i, [128, 512], BF16) for i in range(10)] + [region(512) for _ in range(NWS - 10)]
        q3 = qraw.t.rearrange("p (a b) -> p a b", a=12)
        r3 = rraw.t.rearrange("p (a b) -> p a b", a=14)

        def s3(i):
            return slab[i].t.rearrange("p (a b) -> p a b", a=4)

        hid_ap = arena[:, QW + RW:QW + RW + 4 * SL].bitcast(BF16).rearrange("p (a b) -> p a b", a=32)
        hid = TT(hid_ap, slab[0].bufs + slab[1].bufs + slab[2].bufs + slab[3].bufs)
        gT = TT(slab[1].t.bitcast(BF16)[:, 0:4 * TM].rearrange("p (a b) -> p a b", a=4), slab[1].bufs)
        bonT = TT(slab[4].t.bitcast(BF16)[:, 0:4 * TM].rearrange("p (a b) -> p a b", a=4), slab[4].bufs)

        def w3(i, nu=8):
            return ws[i].t[0:64, :].rearrange("p (a b) -> p a b", a=8)

        banks = [TT(es.enter_context(nc.psum_tensor("bk%d" % i, [128, 512], F32))) for i in range(8)]
        bank_sets = {"main": list(range(8)), "a": list(range(6)), "b": [6, 7]}
        bk_cur = ["main"]
        bk_idx = {"main": 0, "a": 0, "b": 0}

        def bank():
            st_ = bk_cur[0]
            ls_ = bank_sets[st_]
            b = banks[ls_[bk_idx[st_] % len(ls_)]]
            bk_idx[st_] += 1
            return b

        def cm(i, nu):
            return bc(C64.t[:, i:i + 1, :], [64, nu, 64])

        ident = C128.t[:, 0, :]
        ones = C128.t[:, 1, :]
        blk64 = C128.t[:, 2, :]
        bmask = C128.t[:, 3, :]
        id64 = C128.t[0:64, 0, 0:64]

        sem_ld = P.new_sem("dma")
        sem_cast = [P.new_sem("dma") for _ in range(L)]
        sem_wb = [P.new_sem("dma") for _ in range(2)]
        sem_x = P.new_sem("dma")
        sem_out = P.new_sem("dma")
        sem_sm = P.new_sem("dma")
        sem_st = P.new_sem("dma")
        sem_y = P.new_sem("dma")
        sem_wa = [P.new_sem("dma") for _ in range(2)]
        wbfbuf = [[Buf() for _ in range(NG)] for _ in range(L)]
        outbuf = Buf()

        P.dma("sp", C64.t[:].rearrange("p a b -> p (a b)"), c64, sem_ld, W=[C64])
        P.dma("sp", C128.t[:].rearrange("p a b -> p (a b)"), c128, sem_ld, W=[C128])
        P.dma("sp", VEC.t[:].rearrange("p a b -> p (a b)"), vecs, sem_ld, W=[VEC])
        P.dma("sp", ROWC.t[:].rearrange("p a b -> p (a b)"), rowc, sem_ld, W=[ROWC])
        P.dma("sp", WGAB.t[:].rearrange("p a b c -> p (a b c)"), wgab, sem_ld, W=[WGAB])
        P.dma("sp", BADA.t[:].rearrange("p a b -> p (a b)"), bada, sem_ld, W=[BADA])
        P.dma("sp", CT.t[:].rearrange("p a b -> p (a b)"), cT, sem_ld, W=[CT])
        for b_ in (C64, C128, VEC, ROWC, WGAB, BADA, CT):
            b_.bufs[0].w = (sem_ld, sem_ld.v)
        for l in range(L):
            for g in range(NG):
                P.dma("pool", wbf[l, g], wpack[l, g], sem_cast[l], W=[wbfbuf[l][g]])
            for g in range(NG):
                wbfbuf[l][g].w = (sem_cast[l], sem_cast[l].v)
        P.copy("dve", onesb.t[:], ones, R=[C128], W=[onesb])
        P.copy("dve", blkb.t[:], blk64, R=[C128], W=[blkb])
        P.copy("dve", identb.t[:], ident, R=[C128], W=[identb])
        P.copy("dve", WGABb.t[:].rearrange("p a b c -> p (a b c)"), WGAB.t[:].rearrange("p a b c -> p (a b c)"), R=[WGAB], W=[WGABb])
        P.ts("dve", OKA.t[:], VEC.t[:, :, 71:75], -1.0, 1.0, ALU.mult, ALU.add, R=[VEC], W=[OKA])
        P.act(NEA.t[:], ROWC.t[:, :, 0:4], AF.Exp, R=[ROWC], W=[NEA])
        P.ts("dve", NEA.t[:], NEA.t[:], -1.0, None, ALU.mult, R=[NEA], W=[NEA])
        P.act(SCT.t[:], CT.t[:], AF.Silu, R=[CT], W=[SCT])
        wab = [TT(arena[:, QW + RW + i * 4096:QW + RW + (i + 1) * 4096],
                  slab[4 * i].bufs + slab[4 * i + 1].bufs + slab[4 * i + 2].bufs + slab[4 * i + 3].bufs) for i in range(2)]
        gi = 0
        for l in range(L):
            pb = bank()
            for g in range(12):
                wbuf = wab[gi % 2]
                P.dma("sp", wbuf.t, wada[l, g], sem_wa[gi % 2], W=[wbuf])
                gi += 1
                w3v = wbuf.t.rearrange("p (a b) -> p a b", a=8)
                for j in range(4):
                    oc = g * 4 + j
                    for kc in range(8):
                        P.mm(pb.t[:, oc * 3:oc * 3 + 3], w3v[:, kc, j * 128:(j + 1) * 128], SCT.t[:, kc, :],
                             start=(kc == 0), stop=(kc == 7), R=[wbuf, SCT], W=[pb])
            P.tt("dve", MOD.t[:, l, :, :], pb.t[:, 0:144].rearrange("p (a b) -> p a b", a=48),
                 bc(BADA.t[:, l, :].unsqueeze(2), [128, 48, 3]), ALU.add, R=[pb, BADA], W=[MOD])
            for j in (1, 4):
                P.ts("dve", MOD.t[:, l, j * 8:(j + 1) * 8, :], MOD.t[:, l, j * 8:(j + 1) * 8, :], 1.0, None, ALU.add,
                     R=[MOD], W=[MOD])

        tiles = []
        for s in range(2):
            for t in range(SEQ // TM):
                tiles.append((s, t * TM, TM))
        tiles.append((2, 0, DSEQ))
        if _DEBUG_TILES is not None:
            tiles = _DEBUG_TILES
        wseq = []
        for ti in range(len(tiles)):
            for l in range(L):
                for g in range(NG):
                    wseq.append((l, g))
        wst = {"issued": 0, "next": 0}

        def wissue(upto):
            while wst["issued"] < min(upto, len(wseq)):
                i = wst["issued"]
                l, g = wseq[i]
                P.dma("sp", wb[i % 2].t[:], wbf[l, g], sem_wb[i % 2], R=[wbfbuf[l][g]], W=[wb[i % 2]])
                wst["issued"] += 1

        def wnext():
            i = wst["next"]
            wissue(i + 2)
            wst["next"] += 1
            return wb[i % 2]

        rr = {"ev": 0, "el": 0}

        def ev_eng():
            rr["ev"] += 1
            return "act" if rr["ev"] % 2 else "dve"

        def el_eng():
            rr["el"] += 1
            return "dve" if rr["el"] % 2 else "pool"

        def rstd_from(out, in_, scale, eps, tmp, extra_bias=None, R=(), W=()):
            P.act(tmp, in_, AF.Ln, bias=eps, scale=scale, R=R, W=W)
            if extra_bias is None:
                P.act(out, tmp, AF.Exp, scale=-0.5, R=W, W=W)
            else:
                P.act(out, tmp, AF.Exp, scale=-0.5, bias=extra_bias, R=W, W=W)

        def inversion(A, AT, nu):
            iD, iDT, iF, iE = 2, 3, 4, 5
            Av, ATv = w3(A)[:, 0:nu, :], w3(AT)[:, 0:nu, :]
            Dv, DTv, Fv = w3(iD)[:, 0:nu, :], w3(iDT)[:, 0:nu, :], w3(iF)[:, 0:nu, :]
            P.tt("pool", Fv, Av, cm(4, nu), ALU.mult, R=[ws[A], C64], W=[ws[iF]])
            P.stt("pool", Dv, Fv, -1.0, cm(3, nu), ALU.mult, ALU.add, R=[ws[iF], C64], W=[ws[iD]])
            P.tt("pool", Fv, ATv, cm(10, nu), ALU.mult, R=[ws[AT], C64], W=[ws[iF]])
            P.stt("pool", DTv, Fv, -1.0, cm(3, nu), ALU.mult, ALU.add, R=[ws[iF], C64], W=[ws[iDT]])
            for lv in range(1, 6):
                P.tt("pool", w3(iE + lv - 1)[:, 0:nu, :], ATv, cm(10 + lv, nu), ALU.mult, R=[ws[AT], C64], W=[ws[iE + lv - 1]])
            for lv in range(1, 6):
                E = iE + lv - 1
                pF = bank()
                for u in range(nu):
                    P.mm(pF.t[0:64, u * 64:(u + 1) * 64], w3(E)[:, u, :], w3(iD)[:, u, :], R=[ws[E], ws[iD]], W=[pF])
                P.copy("act", Fv, pF.t[0:64, 0:nu * 64].rearrange("p (a b) -> p a b", a=nu), R=[pF], W=[ws[iF]])
                if lv < 5:
                    pG = bank()
                    for u in range(nu):
                        P.mm(pG.t[0:64, u * 64:(u + 1) * 64], w3(iDT)[:, u, :], w3(iF)[:, u, :], R=[ws[iDT], ws[iF]], W=[pG])
                pT = bank()
                for u in range(nu):
                    P.mm(pT.t[0:64, u * 64:(u + 1) * 64], w3(iF)[:, u, :], w3(iDT)[:, u, :], R=[ws[iDT], ws[iF]], W=[pT])
                if lv < 5:
                    P.tt("dve", Dv, Dv, pG.t[0:64, 0:nu * 64].rearrange("p (a b) -> p a b", a=nu), ALU.subtract,
                         R=[ws[iD], pG], W=[ws[iD]])
                P.tt("dve", DTv, DTv, pT.t[0:64, 0:nu * 64].rearrange("p (a b) -> p a b", a=nu), ALU.subtract,
                     R=[ws[iDT], pT], W=[ws[iDT]])
            return iDT

        def layer_norm(T, wcol, bcol, l):
            sqv = arena[:, QW + RW + 5 * SL:QW + RW + 7 * SL].rearrange("p (a b) -> p a b", a=8)
            sqbufs = slab[5].bufs + slab[6].bufs
            P.act(sqv[:, :, 0:T], xT.t[:, :, 0:T], AF.Square, R=[xT], W=sqbufs)
            p1 = bank()
            for kc in range(8):
                P.mm(p1.t[:, 0:T], ones, xT.t[:, kc, 0:T], start=(kc == 0), stop=(kc == 7), R=[C128, xT], W=[p1])
            p2 = bank()
            for kc in range(8):
                P.mm(p2.t[:, 0:T], ones, sqv[:, kc, 0:T], start=(kc == 0), stop=(kc == 7), R=[C128] + sqbufs, W=[p2])
            mean, msq, var, rstd = lnt["mean"], lnt["msq"], lnt["var"], lnt["rstd"]
            P.ts("dve", mean.t[:, 0:T], p1.t[:, 0:T], 1.0 / D, None, ALU.mult, R=[p1], W=[mean])
            P.tt("dve", msq.t[:, 0:T], mean.t[:, 0:T], mean.t[:, 0:T], ALU.mult, R=[mean], W=[msq])
            P.stt("dve", var.t[:, 0:T], p2.t[:, 0:T], 1.0 / D, msq.t[:, 0:T], ALU.mult, ALU.subtract, R=[p2, msq], W=[var])
            rstd_from(rstd.t[:, 0:T], var.t[:, 0:T], 1.0, LN_EPS, msq.t[:, 0:T], R=[var], W=[rstd, msq])
            P.tt("dve", xT.t[:, :, 0:T], xT.t[:, :, 0:T], bc(mean.t[:, 0:T].unsqueeze(1), [128, 8, T]), ALU.subtract,
                 R=[xT, mean], W=[xT])
            P.tt("dve", xT.t[:, :, 0:T], xT.t[:, :, 0:T], bc(rstd.t[:, 0:T].unsqueeze(1), [128, 8, T]), ALU.mult,
                 R=[xT, rstd], W=[xT])
            for kc in range(8):
                P.act(xT.t[:, kc, 0:T], xT.t[:, kc, 0:T], AF.Identity, scale=VEC.t[:, l, wcol + kc:wcol + kc + 1],
                      bias=VEC.t[:, l, bcol + kc:bcol + kc + 1], R=[xT, VEC], W=[xT])

        def layer(s, l, T):
            NCH = T // 64
            nug = NCH * 4
            for kc in range(8):
                P.act(hT.t[:, kc, 0:T], xT.t[:, kc, 0:T], AF.Identity, scale=MOD.t[:, l, 8 + kc, s:s + 1],
                      bias=MOD.t[:, l, kc, s:s + 1], R=[xT, MOD], W=[hT])
            P.copy("pool", q3[:, :, 0:3], cst.t[:, l, :].rearrange("p (a b) -> p a b", a=12), R=[cst], W=[qraw])
            P.copy("pool", r3[:, :, 0:1], sst.t[:, l, :].unsqueeze(2), R=[sst], W=[rraw])
            for g in range(3):
                wbuf = wnext()
                w3v = wbuf.t[:].rearrange("p (a b) -> p a b", a=8)
                for j in range(4):
                    oc = g * 4 + j
                    if oc >= 30:
                        continue
                    pb = bank()
                    for kc in range(8):
                        P.mm(pb.t[:, 0:T], w3v[:, kc, j * 128:(j + 1) * 128], hT.t[:, kc, 0:T], start=(kc == 0),
                             stop=(kc == 7), R=[wbuf, hT], W=[pb])
                    if oc < 12:
                        P.copy(ev_eng(), q3[:, oc, 3:3 + T], pb.t[:, 0:T], R=[pb], W=[qraw])
                    elif oc < 16:
                        P.act(zs.t[:, oc - 12, 0:T], pb.t[:, 0:T], AF.Silu, R=[pb], W=[zs])
                    else:
                        P.copy(ev_eng(), r3[:, oc - 16, 1:1 + T], pb.t[:, 0:T], R=[pb], W=[rraw])
            capW, capG = [], []
            P.cap = capW
            bk_cur[0] = "a"
            for g in range(3, 8):
                wbuf = wnext()
                w3v = wbuf.t[:].rearrange("p (a b) -> p a b", a=8)
                for j in range(4):
                    oc = g * 4 + j
                    if oc >= 30:
                        continue
                    pb = bank()
                    for kc in range(8):
                        P.mm(pb.t[:, 0:T], w3v[:, kc, j * 128:(j + 1) * 128], hT.t[:, kc, 0:T], start=(kc == 0),
                             stop=(kc == 7), R=[wbuf, hT], W=[pb])
                    if oc < 12:
                        P.copy(ev_eng(), q3[:, oc, 3:3 + T], pb.t[:, 0:T], R=[pb], W=[qraw])
                    elif oc < 16:
                        P.act(zs.t[:, oc - 12, 0:T], pb.t[:, 0:T], AF.Silu, R=[pb], W=[zs])
                    else:
                        P.copy(ev_eng(), r3[:, oc - 16, 1:1 + T], pb.t[:, 0:T], R=[pb], W=[rraw])
            P.cap = capG
            bk_cur[0] = "b"
            P.copy("pool", cst.t[:, l, :].rearrange("p (a b) -> p a b", a=12), q3[:, :, T:T + 3], R=[qraw], W=[cst])
            for oc in range(12):
                e = el_eng()
                acc = tmpA[oc % 2]
                P.ts(e, acc.t[:, 0:T], q3[:, oc, 0:T], VEC.t[:, l, oc * 4:oc * 4 + 1], None, ALU.mult, R=[qraw, VEC], W=[acc])
                for j in range(1, 4):
                    P.stt(e, acc.t[:, 0:T], q3[:, oc, j:j + T], VEC.t[:, l, oc * 4 + j:oc * 4 + j + 1], acc.t[:, 0:T],
                          ALU.mult, ALU.add, R=[qraw, VEC, acc], W=[acc])
                if oc < 8:
                    P.act(q3[:, oc, 3:3 + T], acc.t[:, 0:T], AF.Silu, R=[acc], W=[qraw])
                else:
                    P.act(QKB.t[:, oc, 0:T], acc.t[:, 0:T], AF.Silu, R=[acc], W=[QKB])

            def qv(oc):
                return q3[:, oc, 3:3 + T]

            def qb(oc, c):
                return QKB.t[:, oc, c * 64:(c + 1) * 64]

            for oc in range(8):
                sq = sqb.t[:, oc % 4, 0:T]
                P.tt("pool", sq, qv(oc), qv(oc), ALU.mult, R=[qraw], W=[sqb])
                pb = bank()
                P.mm(pb.t[:, 0:T], onesb.t[:], sq, R=[onesb, sqb], W=[pb])
                tb = tmpB[oc % 2]
                rstd_from(tb.t[:, 0:T], pb.t[:, 0:T], 1.0, NORM_EPS, tb.t[:, 0:T],
                          extra_bias=(math.log(128 ** -0.5) if oc < 4 else None), R=[pb], W=[tb])
                P.tt("dve", QKB.t[:, oc, 0:T], qv(oc), tb.t[:, 0:T], ALU.mult, R=[qraw, tb], W=[QKB])
            P.cap = None
            bk_cur[0] = "main"
            P.merge_replay(capW, capG)
            psg = bank()
            for c in range(NCH):
                for kc in range(8):
                    P.mm(psg.t[0:64, c * 8:(c + 1) * 8], hT.t[:, kc, c * 64:(c + 1) * 64], WGABb.t[:, l, kc, :],
                         start=(kc == 0), stop=(kc == 7), R=[hT, WGABb], W=[psg])

            ga = psg.t[0:64, 0:NCH * 8].rearrange("p (a b) -> p a b", a=NCH)[:, :, 0:4]
            gb = psg.t[0:64, 0:NCH * 8].rearrange("p (a b) -> p a b", a=NCH)[:, :, 4:8]

            def smv(n):
                return sm[n].t[:, 0:nug].rearrange("p (a b) -> p a b", a=NCH)

            P.tt("dve", smv("x1"), ga, bc(ROWC.t[:, l:l + 1, 4:8], [64, NCH, 4]), ALU.add, R=[psg, ROWC], W=[sm["x1"]])
            P.act(smv("beta"), gb, AF.Sigmoid, R=[psg], W=[sm["beta"]])
            P.ts("dve", smv("mx"), smv("x1"), 0.0, None, ALU.max, R=[sm["x1"]], W=[sm["mx"]])
            P.stt("dve", smv("ax"), smv("mx"), 2.0, smv("x1"), ALU.mult, ALU.subtract, R=[sm["x1"], sm["mx"]], W=[sm["ax"]])
            P.act(smv("e1"), smv("ax"), AF.Exp, scale=-1.0, R=[sm["ax"]], W=[sm["e1"]])
            P.act(smv("l1"), smv("e1"), AF.Ln, bias=1.0, R=[sm["e1"]], W=[sm["l1"]])
            P.act(smv("lnb"), smv("beta"), AF.Ln, R=[sm["beta"]], W=[sm["lnb"]])
            P.tt("dve", smv("sp"), smv("mx"), smv("l1"), ALU.add, R=[sm["mx"], sm["l1"]], W=[sm["sp"]])
            P.tt("dve", smv("la"), smv("sp"), bc(NEA.t[:, l:l + 1, :], [64, NCH, 4]), ALU.mult, R=[sm["sp"], NEA], W=[sm["la"]])
            pg = bank()
            P.mm(pg.t[0:64, 0:nug], C64.t[:, 2, :], sm["la"].t[:, 0:nug], R=[C64, sm["la"]], W=[pg])
            P.mm(pg.t[:, 64:64 + nug], ones[0:64, :], sm["la"].t[:, 0:nug], R=[C128, sm["la"]], W=[pg])
            P.copy("dve", sm["gcol"].t[:, 0:nug], pg.t[0:64, 0:nug], R=[pg], W=[sm["gcol"]])
            P.act(dl128.t[:, 0:nug], pg.t[:, 64:64 + nug], AF.Exp, R=[pg], W=[dl128])
            P.act(sm["eg"].t[:, 0:nug], sm["gcol"].t[:, 0:nug], AF.Exp, R=[sm["gcol"]], W=[sm["eg"]])
            P.tt("dve", sm["cb"].t[:, 0:nug], sm["eg"].t[:, 0:nug], sm["beta"].t[:, 0:nug], ALU.mult,
                 R=[sm["eg"], sm["beta"]], W=[sm["cb"]])
            P.tt("dve", sm["dd"].t[:, 0:nug], pg.t[0:64, 64:64 + nug], sm["gcol"].t[:, 0:nug], ALU.subtract,
                 R=[pg, sm["gcol"]], W=[sm["dd"]])
            P.act(sm["ed"].t[:, 0:nug], sm["dd"].t[:, 0:nug], AF.Exp, R=[sm["dd"]], W=[sm["ed"]])
            P.tt("dve", sm["gpl"].t[:, 0:nug], sm["gcol"].t[:, 0:nug], sm["lnb"].t[:, 0:nug], ALU.add,
                 R=[sm["gcol"], sm["lnb"]], W=[sm["gpl"]])
            capA, capB = [], []
            P.cap = capA
            bk_cur[0] = "a"
            onG = ONB.t[0:64, :].rearrange("p (c h v) -> p c h v", c=NCHM, h=4)
            Sbv = Sb.t[:, :, :]
            P.copy("act", Sbv, Sg.t[:, l * 4:l * 4 + 4, :], R=[Sg], W=[Sb])
            for c in range(NCH):
                u0 = c * 4
                pG = bank()
                for h in range(4):
                    P.mm(pG.t[:, h * 64:(h + 1) * 64], bc(sm["la"].t[:, u0 + h:u0 + h + 1], [64, 128]), C64.t[:, 2, :],
                         R=[sm["la"], C64], W=[pG])
                eG = ws[12].t[:, 0:256].rearrange("p (a b) -> p a b", a=4)
                P.act(eG, pG.t[:, 0:256].rearrange("p (a b) -> p a b", a=4), AF.Exp, R=[pG], W=[ws[12]])
                qdT = bw[6].t[:, 0:256].rearrange("p (a b) -> p a b", a=4)
                P.tt("dve", qdT, QKB.t[:, 0:4, c * 64:(c + 1) * 64], eG, ALU.mult, R=[QKB, ws[12]], W=[bw[6]])
                dmA = w3(10)[:, 0:4, :]
                dmB = w3(11)[:, 0:4, :]
                for h in range(4):
                    P.ts("dve", dmA[:, h, :], pG.t[0:64, h * 64:(h + 1) * 64], sm["gpl"].t[:, u0 + h:u0 + h + 1], 0.0,
                         ALU.subtract, ALU.max, R=[pG, sm["gpl"]], W=[ws[10]])
                    P.ts("dve", dmB[:, h, :], pG.t[0:64, h * 64:(h + 1) * 64], sm["gcol"].t[:, u0 + h:u0 + h + 1], 0.0,
                         ALU.subtract, ALU.min, R=[pG, sm["gcol"]], W=[ws[11]])
                P.act(dmA, dmA, AF.Exp, scale=-1.0, R=[ws[10]], W=[ws[10]])
                P.act(dmB, dmB, AF.Exp, R=[ws[11]], W=[ws[11]])
                P.tt("pool", dmA, dmA, cm(0, 4), ALU.mult, R=[ws[10], C64], W=[ws[10]])
                P.tt("pool", dmB, dmB, cm(2, 4), ALU.mult, R=[ws[11], C64], W=[ws[11]])
                pK = bank()
                pQ = bank()
                for h in range(4):
                    P.mm(pK.t[0:64, h * 64:(h + 1) * 64], qb(4 + h, c), qb(4 + h, c), R=[QKB], W=[pK])
                    P.mm(pQ.t[0:64, h * 64:(h + 1) * 64], qb(4 + h, c), qb(h, c), R=[QKB], W=[pQ])
                Av = w3(0)[:, 0:4, :]
                P.tt("dve", Av, pK.t[0:64, 0:256].rearrange("p (a b) -> p a b", a=4), dmA, ALU.mult, R=[pK, ws[10]], W=[ws[0]])
                qkT = b3(0, 4, 64)
                P.tt("dve", qkT, pQ.t[0:64, 0:256].rearrange("p (a b) -> p a b", a=4), dmB, ALU.mult, R=[pQ, ws[11]], W=[bw[0]])
                pT = bank()
                pTb = pT.t.bitcast(BF16)
                for h in range(4):
                    P.tr(pTb[0:64, h * 64:(h + 1) * 64], w3(0)[:, h, :], identb.t[0:64, 0:64], R=[ws[0], identb], W=[pT])
                P.copy("act", w3(1)[:, 0:4, :], pTb[0:64, 0:256].rearrange("p (a b) -> p a b", a=4), R=[pT], W=[ws[1]])
                pk = bank()
                pv = bank()
                pkb = pk.t.bitcast(BF16)
                pvb = pv.t.bitcast(BF16)
                for h in range(4):
                    P.tr(pkb[0:64, h * 128:(h + 1) * 128], qb(4 + h, c), identb.t[:], R=[QKB, identb], W=[pk])
                    P.tr(pvb[0:64, h * 128:(h + 1) * 128], qb(8 + h, c), identb.t[:], R=[QKB, identb], W=[pv])
                ktok, RHSv, RHSk, kdec = b3(1, 4, 128), b3(2, 4, 128), b3(3, 4, 128), b3(4, 4, 128)
                P.copy("act", ktok, pkb[0:64, 0:512].rearrange("p (a b) -> p a b", a=4), R=[pk], W=[bw[1]])
                P.tt("dve", RHSv, pvb[0:64, 0:512].rearrange("p (a b) -> p a b", a=4),
                     bc(sm["beta"].t[:, u0:u0 + 4].unsqueeze(2), [64, 4, 128]), ALU.mult, R=[pv, sm["beta"]], W=[bw[2]])
                P.tt("pool", RHSk, ktok, bc(sm["cb"].t[:, u0:u0 + 4].unsqueeze(2), [64, 4, 128]), ALU.mult,
                     R=[bw[1], sm["cb"]], W=[bw[3]])
                P.tt("pool", kdec, ktok, bc(sm["ed"].t[:, u0:u0 + 4].unsqueeze(2), [64, 4, 128]), ALU.mult,
                     R=[bw[1], sm["ed"]], W=[bw[4]])
                iTT = inversion(0, 1, 4)
                TTb = w3(iTT)
                pW = bank()
                for h in range(4):
                    P.mm(pW.t[:, h * 64:(h + 1) * 64], RHSk[:, h, :], TTb[:, h, :], R=[bw[3], ws[iTT]], W=[pW])
                wkTn = bw[7].t[:, 0:256].rearrange("p (a b) -> p a b", a=4)
                P.act(wkTn, pW.t[:, 0:256].rearrange("p (a b) -> p a b", a=4), AF.Copy, scale=-1.0, R=[pW], W=[bw[7]])
                pU = bank()
                for h in range(4):
                    P.mm(pU.t[0:64, h * 128:(h + 1) * 128], TTb[:, h, :], RHSv[:, h, :], start=True, stop=False,
                         R=[ws[iTT], bw[2]], W=[pU])
                    P.mm(pU.t[0:64, h * 128:(h + 1) * 128], wkTn[:, h, :], Sb.t[:, h, :], start=False, stop=True,
                         R=[bw[7], Sb], W=[pU])
                usb = b3(8, 4, 128)
                P.copy("act", usb, pU.t[0:64, :].rearrange("p (a b) -> p a b", a=4), R=[pU], W=[bw[8]])
                pO = bank()
                for h in range(4):
                    P.mm(pO.t[0:64, h * 128:(h + 1) * 128], qdT[:, h, :], Sb.t[:, h, :], start=True, stop=False,
                         R=[bw[6], Sb], W=[pO])
                    P.mm(pO.t[0:64, h * 128:(h + 1) * 128], qkT[:, h, :], usb[:, h, :], start=False, stop=True,
                         R=[bw[0], bw[8]], W=[pO])
                pS = bank()
                for h in range(4):
                    P.mm(pS.t[:, h * 128:(h + 1) * 128], kdec[:, h, :], usb[:, h, :], R=[bw[4], bw[8]], W=[pS])
                P.tt("pool", Sg.t[:, l * 4:l * 4 + 4, :], Sg.t[:, l * 4:l * 4 + 4, :],
                     bc(dl128.t[:, u0:u0 + 4].unsqueeze(2), [128, 4, 128]), ALU.mult, R=[Sg, dl128], W=[Sg])
                P.tt("dve", Sg.t[:, l * 4:l * 4 + 4, :], Sg.t[:, l * 4:l * 4 + 4, :],
                     pS.t[:, :].rearrange("p (a b) -> p a b", a=4), ALU.add, R=[Sg, pS], W=[Sg])
                if c < NCH - 1:
                    P.copy("act", Sbv, Sg.t[:, l * 4:l * 4 + 4, :], R=[Sg], W=[Sb])
                sqt = ws[13].t[0:64, :].rearrange("p (a b) -> p a b", a=4)
                pO3 = pO.t[0:64, :].rearrange("p (a b) -> p a b", a=4)
                P.act(sqt, pO3, AF.Square, R=[pO], W=[ws[13]])
                P.rsum("dve", st8["ssq"].t[:, 0:4], sqt, R=[ws[13]], W=[st8["ssq"]])
                rstd_from(st8["rstd"].t[:, 0:4], st8["ssq"].t[:, 0:4], 1.0 / 128, NORM_EPS, st8["r1"].t[:, 0:4],
                          R=[st8["ssq"]], W=[st8["rstd"], st8["r1"]])
                P.tt("dve", onG[:, c, :, :], pO3, bc(st8["rstd"].t[:, 0:4].unsqueeze(2), [64, 4, 128]), ALU.mult,
                     R=[pO, st8["rstd"]], W=[ONB])
            P.cap = capB
            bk_cur[0] = "b"
            P.dma("sp", W0B.t[:], w0b[l], sem_sm, W=[W0B])
            P.dma("sp", W2A2.t[:], w2a2[l], sem_sm, W=[W2A2])
            P.dma("sp", G2W.t[:], g2w[l], sem_sm, W=[G2W])
            def _pub():
                for b_ in (W0B, W2A2, G2W):
                    b_.bufs[0].w = (sem_sm, sem_sm.v)
            P.defer(_pub)
            P.copy("pool", sst.t[:, l, :].unsqueeze(2), r3[:, :, T:T + 1], R=[rraw], W=[sst])
            for oc in range(14):
                e = el_eng()
                tb = tmpA[oc % 2]
                P.tt(e, tb.t[:, 0:T], r3[:, oc, 0:T], r3[:, oc, 1:T + 1], ALU.subtract, R=[rraw], W=[tb])
                P.stt(e, r3[:, oc, 1:T + 1], tb.t[:, 0:T], VEC.t[:, l, 49 + oc:50 + oc], r3[:, oc, 1:T + 1], ALU.mult, ALU.add,
                      R=[tb, VEC, rraw], W=[rraw])

            def rv(oc0, n=1):
                if n == 1:
                    return r3[:, oc0, 1:1 + T]
                return r3[:, oc0:oc0 + n, 1:1 + T]

            P.act(txw.t[:, 0:T], r3[0:64, 12, 1:1 + T], AF.Tanh, R=[rraw], W=[txw])
            P.act(sxg.t[:, 0:T], rv(13), AF.Sigmoid, R=[rraw], W=[sxg])
            S1, S3, S4, S6, S7, S8 = s3(0), s3(2), s3(3), s3(5), s3(6), s3(7)
            B1, B3, B4, B6, B7, B8 = slab[0], slab[2], slab[3], slab[5], slab[6], slab[7]
            RBv = lambda i: RB.t[:, i, :, 0:T]
            for cc in range(4):
                pb = bank()
                P.mm(pb.t[:, 0:T], W2A2.t[64:128, cc * 128:(cc + 1) * 128], r3[64:128, 12, 1:1 + T], R=[W2A2, rraw], W=[pb])
                P.act(S1[:, cc, 0:T], pb.t[:, 0:T], AF.Sigmoid, bias=VEC.t[:, l, 63 + cc:64 + cc], R=[pb, VEC], W=[B1])
                pb2 = bank()
                P.mm(pb2.t[:, 0:T], G2W.t[:, cc * 128:(cc + 1) * 128], sxg.t[:, 0:T], R=[G2W, sxg], W=[pb2])
                P.copy("dve", gT.t[:, cc, 0:T], pb2.t[:, 0:T], R=[pb2], W=[gT])
            P.copy("act", RBv(6), rv(8, 4), R=[rraw], W=[RB])
            P.tt("dve", S3[:, :, 0:T], rv(4, 4), bc(VEC.t[:, l, 67:71].unsqueeze(2), [128, 4, T]), ALU.mult,
                 R=[rraw, VEC], W=[B3])
            P.tt("dve", sqb.t[:, :, 0:T], S3[:, :, 0:T], S3[:, :, 0:T], ALU.mult, R=[B3], W=[sqb])
            for cc in range(4):
                pb = bank()
                P.mm(pb.t[:, 0:T], blkb.t[:], sqb.t[:, cc, 0:T], R=[blkb, sqb], W=[pb])
                tb = tmpB[cc % 2]
                rstd_from(tb.t[:, 0:T], pb.t[:, 0:T], 1.0, NORM_EPS, tb.t[:, 0:T], R=[pb], W=[tb])
                P.tt("dve", S3[:, cc, 0:T], S3[:, cc, 0:T], tb.t[:, 0:T], ALU.mult, R=[B3, tb], W=[B3])
            for cc in range(4):
                P.ts("dve", S4[:, cc, 0:T], S1[:, cc, 0:T], VEC.t[:, l, 71 + cc:72 + cc], OKA.t[:, l, cc:cc + 1],
                     ALU.mult, ALU.add, R=[B1, VEC, OKA], W=[B4])
            P.tt("dve", rv(4, 4), rv(4, 4), S4[:, :, 0:T], ALU.mult, R=[rraw, B4], W=[rraw])
            P.tt("dve", S1[:, :, 0:T], S1[:, :, 0:T], S3[:, :, 0:T], ALU.mult, R=[B1, B3], W=[B1])
            P.tt("dve", S4[:, :, 0:T], rv(0, 4), rv(4, 4), ALU.mult, R=[rraw], W=[B4])
            P.tt("dve", S4[:, :, 0:T], S4[:, :, 0:T], bc(VEC.t[:, l, 75:79].unsqueeze(2), [128, 4, T]), ALU.mult,
                 R=[B4, VEC], W=[B4])
            for cc in range(4):
                pb = bank()
                P.mm(pb.t[:, 0:T], blk64, S4[:, cc, 0:T], R=[C128, B4], W=[pb])
                P.tt("dve", bonT.t[:, cc, 0:T], pb.t[:, 0:T], rv(8 + cc), ALU.mult, R=[pb, rraw], W=[bonT])
            for c in range(NCH):
                pd = bank()
                P.mm(pd.t[0:64, :], txw.t[0:64, c * 64:(c + 1) * 64], W2A2.t[0:64, :], R=[txw, W2A2], W=[pd])
                P.tt("dve", tmpd.t[:], pd.t[0:64, :], W0B.t[:], ALU.add, R=[pd, W0B], W=[tmpd])
                P.act(tmpd.t[:], tmpd.t[:], AF.Sigmoid, R=[tmpd], W=[tmpd])
                pL = bank()
                for cc in range(4):
                    P.mm(pL.t[:, cc * 64:(cc + 1) * 64], tmpd.t[:, cc * 128:(cc + 1) * 128], C64.t[:, 16, :],
                         R=[tmpd, C64], W=[pL])
                pL3 = pL.t[:, 0:256].rearrange("p (a b) -> p a b", a=4)
                P.act(S6[:, :, c * 64:(c + 1) * 64], pL3, AF.Exp, R=[pL], W=[B6])
                P.act(S7[:, :, c * 64:(c + 1) * 64], pL3, AF.Exp, scale=-1.0, R=[pL], W=[B7])
            P.memset("pool", S8[:, :, 0:T], 1.0, W=[B8])
            for c in range(NCH):
                P.copy("pool", S8[:, :, c * 64 + 1:(c + 1) * 64], S6[:, :, c * 64:(c + 1) * 64 - 1], R=[B6], W=[B8])
                P.copy("pool", WC.t[:, :, c:c + 1], S6[:, :, (c + 1) * 64 - 1:(c + 1) * 64], R=[B6], W=[WC])
            P.stt("dve", RBv(0), S3[:, :, 0:T], -1.0, S8[:, :, 0:T], ALU.mult, ALU.mult, R=[B3, B8], W=[RB])
            P.tt("dve", RBv(3), rv(0, 4), S6[:, :, 0:T], ALU.mult, R=[rraw, B6], W=[RB])
            for c in range(NCH):
                P.tt("pool", S6[:, :, c * 64:(c + 1) * 64], S7[:, :, c * 64:(c + 1) * 64],
                     bc(WC.t[:, :, c:c + 1], [128, 4, 64]), ALU.mult, R=[B7, WC], W=[B6])
            P.tt("dve", RBv(4), S1[:, :, 0:T], S6[:, :, 0:T], ALU.mult, R=[B1, B6], W=[RB])
            P.tt("dve", RBv(5), rv(4, 4), S6[:, :, 0:T], ALU.mult, R=[rraw, B6], W=[RB])
            P.tt("dve", RBv(1), S1[:, :, 0:T], S7[:, :, 0:T], ALU.mult, R=[B1, B7], W=[RB])
            P.tt("dve", RBv(2), rv(4, 4), S7[:, :, 0:T], ALU.mult, R=[rraw, B7], W=[RB])
            P.cap = None
            bk_cur[0] = "main"
            P.merge_replay(capA, capB)
            for h in range(4):
                pX = bank()
                pXb = pX.t.bitcast(BF16)
                for c in range(NCH):
                    P.tr(pXb[:, c * 64:(c + 1) * 64], onG[:, c, h, :], identb.t[0:64, 0:64], R=[ONB, identb], W=[pX])
                P.stt("dve", mixT.t[:, h, 0:T], pXb[:, 0:T], VEC.t[:, l, 48:49], zs.t[:, h, 0:T], ALU.mult, ALU.mult,
                      R=[pX, VEC, zs], W=[mixT])

            onR = ONB.t[0:64, :].rearrange("p (c h v) -> p c h v", c=NCHM, h=8)
            Mbv = Mbb.t[:, :, :]
            P.copy("act", Mbv, Mb.t[:, l * 4:l * 4 + 4, :], R=[Mb], W=[Mbb])
            for c in range(NCH):
                cs = slice(c * 64, (c + 1) * 64)
                pA, pAT, pAk, pRb, pRk = bank(), bank(), bank(), bank(), bank()
                for hh in range(8):
                    cc, half = hh // 2, hh % 2
                    pr = slice(half * 64, half * 64 + 64)
                    at = RB.t[pr, 0, cc, cs]
                    bt = RB.t[pr, 1, cc, cs]
                    kt = RB.t[pr, 2, cc, cs]
                    rt = RB.t[pr, 3, cc, cs]
                    hs = slice(hh * 64, (hh + 1) * 64)
                    P.mm(pA.t[0:64, hs], at, bt, R=[RB], W=[pA])
                    P.mm(pAT.t[0:64, hs], bt, at, R=[RB], W=[pAT])
                    P.mm(pAk.t[0:64, hs], kt, at, R=[RB], W=[pAk])
                    P.mm(pRb.t[0:64, hs], bt, rt, R=[RB], W=[pRb])
                    P.mm(pRk.t[0:64, hs], kt, rt, R=[RB], W=[pRk])

                def p8(b):
                    return b.t[0:64, :].rearrange("p (a b) -> p a b", a=8)

                AakT, ArbT, ArkT = b3(0, 8, 64), b3(1, 8, 64), b3(2, 8, 64)
                P.stt("dve", w3(0), p8(pA), -1.0, cm(0, 8), ALU.mult, ALU.mult, R=[pA, C64], W=[ws[0]])
                P.stt("dve", w3(1), p8(pAT), -1.0, cm(1, 8), ALU.mult, ALU.mult, R=[pAT, C64], W=[ws[1]])
                P.tt("dve", AakT, p8(pAk), cm(1, 8), ALU.mult, R=[pAk, C64], W=[bw[0]])
                P.tt("dve", ArbT, p8(pRb), cm(2, 8), ALU.mult, R=[pRb, C64], W=[bw[1]])
                P.tt("dve", ArkT, p8(pRk), cm(2, 8), ALU.mult, R=[pRk, C64], W=[bw[2]])
                pB, pKh, pV = bank(), bank(), bank()
                pBb, pKhb, pVb = pB.t.bitcast(BF16), pKh.t.bitcast(BF16), pV.t.bitcast(BF16)
                for cc in range(4):
                    P.tr(pBb[0:64, cc * 128:(cc + 1) * 128], RB.t[:, 4, cc, cs], identb.t[:], R=[RB, identb], W=[pB])
                    P.tr(pKhb[0:64, cc * 128:(cc + 1) * 128], RB.t[:, 5, cc, cs], identb.t[:], R=[RB, identb], W=[pKh])
                    P.tr(pVb[0:64, cc * 128:(cc + 1) * 128], RB.t[:, 6, cc, cs], identb.t[:], R=[RB, identb], W=[pV])
                Bht, Kht, Vt = bw[3].t[0:64, :], bw[4].t[0:64, :], bw[6].t[0:64, :]
                P.copy("act", Bht, pBb[0:64, 0:512], R=[pB], W=[bw[3]])
                P.copy("act", Kht, pKhb[0:64, 0:512], R=[pKh], W=[bw[4]])
                P.copy("act", Vt, pVb[0:64, 0:512], R=[pV], W=[bw[6]])
                iTT = inversion(0, 1, 8)
                TTb = w3(iTT)
                Xsb, Usb = bw[7].t[0:64, :], bw[8].t[0:64, :]
                pX = bank()
                for hh in range(8):
                    cc, half = hh // 2, hh % 2
                    hs = slice(hh * 64, (hh + 1) * 64)
                    P.mm(pX.t[0:64, hs], RB.t[:, 0, cc, cs], Mbb.t[:, cc, half * 64:(half + 1) * 64], start=True, stop=False,
                         R=[RB, Mbb], W=[pX])
                    P.mm(pX.t[0:64, hs], AakT[:, hh, :], Vt[:, hs], start=False, stop=True, R=[bw[0], bw[6]], W=[pX])
                P.copy("act", Xsb, pX.t[0:64, :], R=[pX], W=[bw[7]])
                pU = bank()
                for hh in range(8):
                    hs = slice(hh * 64, (hh + 1) * 64)
                    P.mm(pU.t[0:64, hs], TTb[:, hh, :], Xsb[:, hs], R=[ws[iTT], bw[7]], W=[pU])
                P.copy("dve", Usb, pU.t[0:64, :], R=[pU], W=[bw[8]])
                pO = bank()
                for hh in range(8):
                    cc, half = hh // 2, hh % 2
                    hs = slice(hh * 64, (hh + 1) * 64)
                    P.mm(pO.t[0:64, hs], RB.t[:, 3, cc, cs], Mbb.t[:, cc, half * 64:(half + 1) * 64],
                         start=True, stop=False, R=[RB, Mbb], W=[pO])
                    P.mm(pO.t[0:64, hs], ArbT[:, hh, :], Usb[:, hs], start=False, stop=False, R=[bw[1], bw[8]], W=[pO])
                    P.mm(pO.t[0:64, hs], ArkT[:, hh, :], Vt[:, hs], start=False, stop=True, R=[bw[2], bw[6]], W=[pO])
                pM = bank()
                for cc in range(4):
                    ps_ = slice(cc * 128, (cc + 1) * 128)
                    P.mm(pM.t[:, ps_], Bht[:, ps_], Usb[:, ps_], start=True, stop=False, R=[bw[3], bw[8]], W=[pM])
                    P.mm(pM.t[:, ps_], Kht[:, ps_], Vt[:, ps_], start=False, stop=True, R=[bw[4], bw[6]], W=[pM])
                tmpM = ws[10].t[:, :].rearrange("p (a b) -> p a b", a=4)
                P.tt("dve", tmpM, pM.t[:, :].rearrange("p (a b) -> p a b", a=4), bc(C128.t[:, 3:4, :], [128, 4, 128]), ALU.mult,
                     R=[pM, C128], W=[ws[10]])
                Mv = Mb.t[:, l * 4:l * 4 + 4, :]
                P.tt("pool", Mv, Mv, bc(WC.t[:, :, c:c + 1], [128, 4, 128]), ALU.mult, R=[Mb, WC], W=[Mb])
                P.tt("dve", Mv, Mv, tmpM, ALU.add, R=[Mb, ws[10]], W=[Mb])
                if c < NCH - 1:
                    P.copy("act", Mbv, Mv, R=[Mb], W=[Mbb])
                pO3 = p8(pO)
                sqt = w3(13)
                cen = w3(14)
                P.rsum("dve", st8["sum"].t[:], pO3, R=[pO], W=[st8["sum"]])
                P.act(sqt, pO3, AF.Square, R=[pO], W=[ws[13]])
                P.rsum("dve", st8["ssq"].t[:], sqt, R=[ws[13]], W=[st8["ssq"]])
                P.ts("dve", st8["mean"].t[:], st8["sum"].t[:], 1.0 / 64, None, ALU.mult, R=[st8["sum"]], W=[st8["mean"]])
                P.tt("dve", st8["msq"].t[:], st8["mean"].t[:], st8["mean"].t[:], ALU.mult, R=[st8["mean"]], W=[st8["msq"]])
                P.stt("dve", st8["var"].t[:], st8["ssq"].t[:], 1.0 / 64, st8["msq"].t[:], ALU.mult, ALU.subtract,
                      R=[st8["ssq"], st8["msq"]], W=[st8["var"]])
                rstd_from(st8["rstd"].t[:], st8["var"].t[:], 1.0, GN_EPS, st8["r1"].t[:], R=[st8["var"]],
                          W=[st8["rstd"], st8["r1"]])
                P.tt("dve", cen, pO3, bc(st8["mean"].t[:].unsqueeze(2), [64, 8, 64]), ALU.subtract, R=[pO, st8["mean"]], W=[ws[14]])
                P.tt("pool", onR[:, c, :, :], cen, bc(st8["rstd"].t[:].unsqueeze(2), [64, 8, 64]), ALU.mult,
                     R=[ws[14], st8["rstd"]], W=[ONB])
            for cc in range(4):
                pX = bank()
                pXb = pX.t.bitcast(BF16)
                for c in range(NCH):
                    P.tr(pXb[:, c * 64:(c + 1) * 64], onR[:, c, 2 * cc:2 * cc + 2, :].rearrange("p a b -> p (a b)"),
                         identb.t[0:64, 0:64], R=[ONB, identb], W=[pX])
                tb = tmpB[cc % 2]
                P.act(tb.t[:, 0:T], pXb[:, 0:T], AF.Identity, scale=VEC.t[:, l, 79 + cc:80 + cc], bias=VEC.t[:, l, 83 + cc:84 + cc],
                      R=[pX, VEC], W=[tb])
                P.tt("dve", tb.t[:, 0:T], tb.t[:, 0:T], bonT.t[:, cc, 0:T], ALU.add, R=[tb, bonT], W=[tb])
                P.tt("pool", mixT.t[:, 4 + cc, 0:T], tb.t[:, 0:T], gT.t[:, cc, 0:T], ALU.mult, R=[tb, gT], W=[mixT])

            P.ts("dve", xT.t[:, :, 0:T], xT.t[:, :, 0:T], ALPHA, None, ALU.mult, R=[xT], W=[xT])
            for g in range(2):
                wbuf = wnext()
                w3v = wbuf.t[:].rearrange("p (a b) -> p a b", a=8)
                for j in range(4):
                    oc = g * 4 + j
                    pb = bank()
                    for kc in range(8):
                        P.mm(pb.t[:, 0:T], w3v[:, kc, j * 128:(j + 1) * 128], mixT.t[:, kc, 0:T], start=(kc == 0), stop=(kc == 7),
                             R=[wbuf, mixT], W=[pb])
                    P.stt("dve", xT.t[:, oc, 0:T], pb.t[:, 0:T], MOD.t[:, l, 16 + oc, s:s + 1], xT.t[:, oc, 0:T], ALU.mult, ALU.add,
                          R=[pb, MOD, xT], W=[xT])
            layer_norm(T, 87, 95, l)
            for kc in range(8):
                P.act(hT.t[:, kc, 0:T], xT.t[:, kc, 0:T], AF.Identity, scale=MOD.t[:, l, 32 + kc, s:s + 1],
                      bias=MOD.t[:, l, 24 + kc, s:s + 1], R=[xT, MOD], W=[hT])
            P.ts("dve", xT.t[:, :, 0:T], xT.t[:, :, 0:T], ALPHA, None, ALU.mult, R=[xT], W=[xT])
            for g in range(8):
                wbuf = wnext()
                w3v = wbuf.t[:].rearrange("p (a b) -> p a b", a=8)
                for j in range(4):
                    oc = g * 4 + j
                    pb = bank()
                    for kc in range(8):
                        P.mm(pb.t[:, 0:T], w3v[:, kc, j * 128:(j + 1) * 128], hT.t[:, kc, 0:T], start=(kc == 0), stop=(kc == 7),
                             R=[wbuf, hT], W=[pb])
                    tb = tmpA[oc % 2]
                    P.act(tb.t[:, 0:T], pb.t[:, 0:T], AF.Relu, R=[pb], W=[tb])
                    P.tt("pool", hid.t[:, oc, 0:T], tb.t[:, 0:T], tb.t[:, 0:T], ALU.mult, R=[tb], W=[hid])
            for oc in range(8):
                wbuf = wnext()
                w32 = wbuf.t[:].rearrange("p (a b) -> p a b", a=32)
                pb = bank()
                for kc in range(32):
                    P.mm(pb.t[:, 0:T], w32[:, kc, :], hid.t[:, kc, 0:T], start=(kc == 0), stop=(kc == 31), R=[wbuf, hid], W=[pb])
                P.stt("dve", xT.t[:, oc, 0:T], pb.t[:, 0:T], MOD.t[:, l, 40 + oc, s:s + 1], xT.t[:, oc, 0:T], ALU.mult, ALU.add,
                      R=[pb, MOD, xT], W=[xT])
            layer_norm(T, 103, 111, l)

        for (s, t0, T) in tiles:
            if s < 2:
                src = xp[s].rearrange("(a p) t -> p a t", p=128)[:, :, t0:t0 + T]
            else:
                src = xs.rearrange("(a p) t -> p a t", p=128)
            P.dma("sp", xT.t[:, :, 0:T], src, sem_x, W=[xT])
            if t0 == 0:
                if s < 2:
                    P.memset("pool", Sg.t[:], 0.0, W=[Sg])
                    P.memset("pool", Mb.t[:], 0.0, W=[Mb])
                    P.memset("pool", cst.t[:], 0.0, W=[cst])
                    P.memset("pool", sst.t[:], 0.0, W=[sst])
                else:
                    for l in range(L):
                        P.dma("sp", Sg.t[:, l * 4:l * 4 + 4, :].rearrange("p a b -> p (a b)"), sgdn[l], sem_st,
                              W=([Sg, Mb, cst, sst] if l == 0 else [Buf()]))
                        P.dma("sp", Mb.t[:, l * 4:l * 4 + 4, :].rearrange("p a b -> p (a b)"), srw[l], sem_st, W=[Buf()])
                        P.dma("sp", cst.t[:, l, :], sconv[l], sem_st, W=[Buf()])
                        P.dma("sp", sst.t[:, l, :], sshift[l], sem_st, W=[Buf()])
                    for b_ in (Sg, Mb, cst, sst):
                        b_.bufs[0].w = (sem_st, sem_st.v)
            for l in range(L):
                layer(s, l, T)
            if s < 2:
                dst = ypT[s].rearrange("(a p) t -> p a t", p=128)[:, :, t0:t0 + T]
            else:
                dst = ysT.rearrange("(a p) t -> p a t", p=128)
            P.dma("sp", dst, xT.t[:, :, 0:T], sem_y, R=[xT], W=[Buf()])
            last = (t0 + T == (SEQ if s < 2 else DSEQ))
            if last:
                for l in range(L):
                    P.dma("sp", o_gdn[s, l], Sg.t[:, l * 4:l * 4 + 4, :].rearrange("p a b -> p (a b)"), sem_out, R=[Sg], W=[Buf()])
                    P.dma("sp", o_rw[s, l], Mb.t[:, l * 4:l * 4 + 4, :].rearrange("p a b -> p (a b)"), sem_out, R=[Mb], W=[Buf()])
                    P.dma("sp", o_conv[s, l], cst.t[:, l, :], sem_out, R=[cst], W=[Buf()])
                    P.dma("sp", o_shift[s, l], sst.t[:, l, :], sem_out, R=[sst], W=[Buf()])
                for b_ in (Sg, Mb, cst, sst):
                    b_.bufs[0].r[sem_out] = sem_out.v
        P.wait_on("sp", sem_out, sem_out.v)
        P.wait_on("sp", sem_y, sem_y.v)

        block = es.enter_context(nc.Block())

        def replay(eng, items):
            for it in items:
                if it[0] == 0:
                    eng.wait_ge(it[1].h, it[2])
                else:
                    it[1](eng).then_inc(it[2].h, it[3])

        @block.tensor
        def _(e):
            replay(e, P.q["pe"])

        @block.scalar
        def _(e):
            replay(e, P.q["act"])

        @block.vector
        def _(e):
            replay(e, P.q["dve"])

        @block.gpsimd
        def _(e):
            replay(e, P.q["pool"])

        @block.sync
        def _(e):
            replay(e, P.q["sp"])

        print("program ops:", P.nops, "sems:", P.nsem, {k: len(v) for k, v in P.q.items()})
    return nc


def _consts():
    idx = np.arange(64)
    t = idx[:, None]
    i = idx[None, :]
    m = np.zeros((17, 64, 64), np.float32)
    m[0] = (t > i)
    m[1] = (t < i)
    m[2] = (t <= i)
    m[3] = np.eye(64)
    for lv in range(6):
        bs = 1 << lv
        ml = ((t // (2 * bs)) == (i // (2 * bs))) & ((t // bs) != (i // bs)) & (t > i)
        m[4 + lv] = ml
        m[10 + lv] = ml.T
    m[16] = -math.exp(-0.5) * (t <= i)
    c64 = np.ascontiguousarray(m.transpose(1, 0, 2)).reshape(64, 17 * 64)
    c128 = np.zeros((128, 4, 128), np.float32)
    c128[:, 0, :] = np.eye(128)
    c128[:, 1, :] = 1.0
    c128[0:64, 2, 0:64] = 1.0
    c128[64:128, 2, 64:128] = 1.0
    c128[:, 3, :] = c128[:, 2, :]
    return c64, c128.reshape(128, 512)


_NC = None
_DEBUG_TILES = None


def kernel(x_prompt, x_sample, c_prompt, c_sample, state_gdn, state_gdn_conv, state_rwkv,
           state_rwkv_shift, w_ada, b_ada, w_in, gdn_conv_w, gdn_a_log, gdn_dt_bias, gdn_norm_w,
           rwkv_mu, rwkv_w0, rwkv_w2, rwkv_a0, rwkv_a2, rwkv_g2, rwkv_kk, rwkv_ka, rwkv_rk,
           rwkv_ln_w, rwkv_ln_b, w_out, ln1_w, ln1_b, w_ff1, w_ff2, ln2_w, ln2_b):
    global _NC
    f = np.float32
    A = lambda a: np.ascontiguousarray(np.asarray(a, dtype=f))
    x_prompt, x_sample = A(x_prompt), A(x_sample)
    w_in = A(w_in)
    c64, c128 = _consts()
    wpack = np.zeros((L, NG, 128, 4096), f)
    for l in range(L):
        wi = np.concatenate([w_in[l][:, 0:2048], w_in[l][:, 2056:3848], np.zeros((D, 256), f)], axis=1)
        wpack[l, 0:8] = wi.reshape(8, 128, 8, 512).transpose(2, 1, 0, 3).reshape(8, 128, 4096)
        wo = A(w_out[l])
        wpack[l, 8:10] = wo.reshape(8, 128, 2, 512).transpose(2, 1, 0, 3).reshape(2, 128, 4096)
        w1 = A(w_ff1[l])
        wpack[l, 10:18] = w1.reshape(8, 128, 8, 512).transpose(2, 1, 0, 3).reshape(8, 128, 4096)
        w2 = A(w_ff2[l])
        wpack[l, 18:26] = w2.reshape(32, 128, 8, 128).transpose(2, 1, 0, 3).reshape(8, 128, 4096)
    wgab = np.zeros((128, L, 8, 8), f)
    for l in range(L):
        wgab[:, l] = w_in[l][:, 2048:2056].reshape(8, 128, 8).transpose(1, 0, 2)
    wgab = wgab.reshape(128, L * 64)
    wada = A(w_ada).reshape(L, 8, 128, 12, 512).transpose(0, 3, 2, 1, 4).reshape(L, 12, 128, 4096)
    wada = np.ascontiguousarray(wada)
    bada = np.ascontiguousarray(A(b_ada).reshape(L, 48, 128).transpose(2, 0, 1)).reshape(128, L * 48)
    vecs = np.zeros((128, L, NV), f)

    def pc(v, n):
        return A(v).reshape(n, 128).T

    for l in range(L):
        cw = A(gdn_conv_w[l])
        vecs[:, l, 0:48] = cw.reshape(4, 12, 128).transpose(2, 1, 0).reshape(128, 48)
        vecs[:, l, 48] = A(gdn_norm_w[l])
        vecs[:, l, 49:63] = pc(rwkv_mu[l], 14)
        vecs[:, l, 63:67] = pc(rwkv_a0[l], 4)
        vecs[:, l, 67:71] = pc(rwkv_kk[l], 4)
        vecs[:, l, 71:75] = pc(rwkv_ka[l], 4)
        vecs[:, l, 75:79] = pc(rwkv_rk[l], 4)
        vecs[:, l, 79:83] = pc(rwkv_ln_w[l], 4)
        vecs[:, l, 83:87] = pc(rwkv_ln_b[l], 4)
        vecs[:, l, 87:95] = pc(ln1_w[l], 8)
        vecs[:, l, 95:103] = pc(ln1_b[l], 8)
        vecs[:, l, 103:111] = pc(ln2_w[l], 8)
        vecs[:, l, 111:119] = pc(ln2_b[l], 8)
    vecs = vecs.reshape(128, L * NV)
    rowc = np.zeros((64, L, 8), f)
    rowc[:, :, 0:4] = A(gdn_a_log)[None]
    rowc[:, :, 4:8] = A(gdn_dt_bias)[None]
    rowc = rowc.reshape(64, L * 8)
    w0b = np.ascontiguousarray(np.broadcast_to(A(rwkv_w0)[:, None, :], (L, 64, 512)))
    w2a2 = np.ascontiguousarray(np.concatenate([A(rwkv_w2), A(rwkv_a2)], axis=1))
    g2w = A(rwkv_g2)
    shared = dict(wpack=wpack, wgab=wgab, wada=wada, bada=bada, vecs=vecs, rowc=rowc, w0b=w0b, w2a2=w2a2, g2w=g2w,
                  c64=c64, c128=c128)
    in_maps = []
    for c in range(8):
        m = dict(shared)
        m["xp"] = np.ascontiguousarray(x_prompt[2 * c:2 * c + 2].transpose(0, 2, 1))
        m["xs"] = np.ascontiguousarray(x_sample[c].T)
        cc = np.stack([A(c_prompt[2 * c]), A(c_prompt[2 * c + 1]), A(c_sample[c])], axis=1)
        m["cT"] = np.ascontiguousarray(cc.reshape(8, 128, 3).transpose(1, 0, 2)).reshape(128, 24)
        sg = A(state_gdn[:, c])
        m["sgdn"] = np.ascontiguousarray(sg.transpose(0, 2, 1, 3)).reshape(L, 128, 512)
        sc = A(state_gdn_conv[:, c])
        m["sconv"] = np.ascontiguousarray(sc.reshape(L, 3, 12, 128).transpose(0, 3, 2, 1)).reshape(L, 128, 36)
        sr = A(state_rwkv[:, c])
        mb = np.zeros((L, 128, 4, 128), f)
        for hh in range(8):
            cc_, half = hh // 2, hh % 2
            mb[:, half * 64:(half + 1) * 64, cc_, half * 64:(half + 1) * 64] = sr[:, hh].transpose(0, 2, 1)
        m["srw"] = mb.reshape(L, 128, 512)
        ss = A(state_rwkv_shift[:, c])
        m["sshift"] = np.ascontiguousarray(ss.reshape(L, 14, 128).transpose(0, 2, 1))
        in_maps.append(m)
    if _NC is None:
        _NC = build_nc()
    res = run_bass_kernel_spmd(_NC, in_maps, core_ids=list(range(8)))
    R = res.results
    B, DB = 16, 8
    y_prompt = np.zeros((B, SEQ, D), f)
    y_sample = np.zeros((DB, DSEQ, D), f)
    outs = {}
    for nm, nb in (("p", B), ("s", DB)):
        outs[nm] = [np.zeros((L, nb, 4, 128, 128), f), np.zeros((L, nb, 3, 1536), f),
                    np.zeros((L, nb, 8, 64, 64), f), np.zeros((L, nb, 1792), f)]
    for c in range(8):
        r = R[c]
        y_prompt[2 * c:2 * c + 2] = r["ypT"].transpose(0, 2, 1)
        y_sample[c] = r["ysT"].T
        for s in range(3):
            nm, b = ("p", 2 * c + s) if s < 2 else ("s", c)
            og = r["o_gdn"][s].reshape(L, 128, 4, 128).transpose(0, 2, 1, 3)
            outs[nm][0][:, b] = og
            oc = r["o_conv"][s].reshape(L, 128, 12, 3).transpose(0, 3, 2, 1).reshape(L, 3, 1536)
            outs[nm][1][:, b] = oc
            orw = r["o_rw"][s].reshape(L, 128, 4, 128)
            for hh in range(8):
                cc_, half = hh // 2, hh % 2
                outs[nm][2][:, b, hh] = orw[:, half * 64:(half + 1) * 64, cc_, half * 64:(half + 1) * 64].transpose(0, 2, 1)
            osh = r["o_shift"][s].transpose(0, 2, 1).reshape(L, 1792)
            outs[nm][3][:, b] = osh
    return (y_prompt, y_sample, outs["p"][0], outs["p"][1], outs["p"][2], outs["p"][3],
            outs["s"][0], outs["s"][1], outs["s"][2], outs["s"][3])
```
